# Optimizing a Trainium2 kernel written in Bass

```python
import jax, jax.numpy as jnp
from jax import lax
import numpy as np

D_MODEL = 2048
BATCH = 2
SEQ = 16384
DEPTH = 2
DEC_BATCH = 16
DEC_SEQ = 16
PAST_LEN = 1024

CHUNK = 64
QBLK = 128
N_BRANCH = 4
HA = 4
DHA = 128
WA = HA * DHA
WB = 512
GB = 4
GC = WB // GB
GM_CHUNK = 128
HC = 4
NOPE = 128
ROPE_DIM = 64
VDIM = 128
Q_RANK = 384
KV_RANK = 256
WC = HC * VDIM
WD = 512
CONV_W = 3
ROPE_BASE = 10000.0
FORGET_BIAS = 3.0
EPS = 1e-6
NEG_INF = -1e30
SEG_SIZES = (WA, WA, WA, HA, WA, WB, WB, WB, Q_RANK, KV_RANK, ROPE_DIM, WC, WD, WD, WD, WD, N_BRANCH * D_MODEL)
N_IN = sum(SEG_SIZES)

kernel_name = "hybrid_fox_gmlp_mla_shortconv_stream_step"


def _split_points():
    pts, acc = [], 0
    for s in SEG_SIZES[:-1]:
        acc += s
        pts.append(acc)
    return pts


def _rmsnorm(x, g):
    xf = x.astype(jnp.float32)
    y = xf * lax.rsqrt(jnp.mean(xf * xf, axis=-1, keepdims=True) + EPS)
    return (y * g.astype(jnp.float32)).astype(x.dtype)


def _layernorm(x, g, b):
    xf = x.astype(jnp.float32)
    mu = jnp.mean(xf, axis=-1, keepdims=True)
    xc = xf - mu
    var = jnp.mean(xc * xc, axis=-1, keepdims=True)
    return (xc * lax.rsqrt(var + EPS) * g.astype(jnp.float32) + b.astype(jnp.float32)).astype(x.dtype)


def _rope(x, pos):
    half = x.shape[-1] // 2
    inv = ROPE_BASE ** (-jnp.arange(half, dtype=jnp.float32) / half)
    ang = pos.astype(jnp.float32)[:, None] * inv[None, :]
    shape = (1, pos.shape[0]) + (1,) * (x.ndim - 3) + (half,)
    cos, sin = jnp.cos(ang).reshape(shape), jnp.sin(ang).reshape(shape)
    x1 = x[..., :half].astype(jnp.float32)
    x2 = x[..., half:].astype(jnp.float32)
    return jnp.concatenate([x1 * cos - x2 * sin, x2 * cos + x1 * sin], axis=-1).astype(x.dtype)


def _attend(s, mask, v):
    p = jax.nn.softmax(jnp.where(mask, s, NEG_INF), axis=-1)
    return jnp.einsum("bhqk,bkhd->bqhd", p.astype(v.dtype), v)


def _fox_core(q, k, v, cq, ckT, qpos, kpos):
    s = jnp.einsum("bqhd,bkhd->bhqk", q, k, preferred_element_type=jnp.float32) * (DHA ** -0.5)
    s = s + (jnp.transpose(cq, (0, 2, 1))[..., :, None] - ckT[..., None, :])
    mask = kpos[None, :] <= qpos[:, None]
    return _attend(s, mask, v)


def _mla_core(q_nope, q_rope, k_nope, v, k_rope, qpos, kpos):
    s = (jnp.einsum("bqhd,bkhd->bhqk", q_nope, k_nope, preferred_element_type=jnp.float32)
         + jnp.einsum("bqhr,bkr->bhqk", q_rope, k_rope, preferred_element_type=jnp.float32))
    s = s * ((NOPE + ROPE_DIM) ** -0.5)
    mask = (kpos // CHUNK)[None, :] <= (qpos // CHUNK)[:, None]
    return _attend(s, mask, v)


def _query_blocks(fn, qpos, *qs):
    S = qpos.shape[0]
    nb = S // QBLK
    blk = lambda a: jnp.moveaxis(a.reshape((a.shape[0], nb, QBLK) + a.shape[2:]), 1, 0)
    out = lax.map(lambda args: fn(*args), (qpos.reshape(nb, QBLK),) + tuple(blk(a) for a in qs))
    out = jnp.moveaxis(out, 0, 1)
    return out.reshape((out.shape[0], S) + out.shape[3:])


def _layer(x, pos, lp, cache):
    B, T, _ = x.shape
    keep = min(T, PAST_LEN)
    h = _rmsnorm(x, lp["norm_g"])
    z = jnp.einsum("btd,dn->btn", h, lp["w_in"])
    (a_q, a_k, a_v, a_f, a_g, b_u, b_v, b_g, c_q, c_kv, c_kr, c_g,
     d_b, d_c, d_h, d_g, m_g) = jnp.split(z, _split_points(), axis=-1)

    q = a_q.reshape(B, T, HA, DHA)
    k = a_k.reshape(B, T, HA, DHA)
    v = a_v.reshape(B, T, HA, DHA)
    logf = jax.nn.log_sigmoid(a_f.astype(jnp.float32) + lp["fox_fb"].astype(jnp.float32))
    if cache is None:
        kpos = jnp.arange(T, dtype=jnp.int32)
        c = jnp.cumsum(logf, axis=1)
        ckT = jnp.transpose(c, (0, 2, 1))
        o_a = _query_blocks(lambda qp, qb, cb: _fox_core(qb, k, v, cb, ckT, qp, kpos), pos, q, c)
        st_fox = (k[:, T - keep:], v[:, T - keep:], logf[:, T - keep:])
    else:
        P = cache["fox_k"].shape[1]
        k_all = jnp.concatenate([cache["fox_k"], k], axis=1)
        v_all = jnp.concatenate([cache["fox_v"], v], axis=1)
        c = jnp.cumsum(jnp.concatenate([cache["fox_logf"].astype(jnp.float32), logf], axis=1), axis=1)
        o_a = _fox_core(q, k_all, v_all, c[:, P:], jnp.transpose(c, (0, 2, 1)), pos,
                        jnp.arange(P + T, dtype=jnp.int32))
        st_fox = (k, v, logf)
    o_a = o_a.reshape(B, T, WA)

    vn = _layernorm(b_v, lp["gm_ln_g"], lp["gm_ln_b"])
    wm = jnp.where(jnp.tril(jnp.ones((GM_CHUNK, GM_CHUNK), dtype=bool)), lp["gm_ws"], 0.0)
    if cache is None:
        n = T // GM_CHUNK
        vc = vn.reshape(B, n, GM_CHUNK, GB, GC)
        mix = jnp.einsum("gts,bnsgc->bntgc", wm, vc) + lp["gm_bs"].T[None, None, :, :, None]
        st_gm = vn[:, T - GM_CHUNK:]
    else:
        vc = vn.reshape(B, T, GB, GC)
        mix = jnp.einsum("gts,bsgc->btgc", wm[:, :T, :T], vc) + lp["gm_bs"][:, :T].T[None, :, :, None]
        st_gm = vn
    o_b = b_u * mix.reshape(B, T, WB)

    cq = _rmsnorm(c_q, lp["mla_q_norm_g"])
    qf = jnp.einsum("btr,rn->btn", cq, lp["mla_wq_b"]).reshape(B, T, HC, NOPE + ROPE_DIM)
    q_nope = qf[..., :NOPE]
    q_rope = _rope(qf[..., NOPE:], pos)
    ckv = _rmsnorm(c_kv, lp["mla_kv_norm_g"])
    kr = _rope(c_kr, pos)
    if cache is None:
        ckv_all, kr_all = ckv, kr
        kpos = jnp.arange(T, dtype=jnp.int32)
    else:
        P = cache["mla_ckv"].shape[1]
        ckv_all = jnp.concatenate([cache["mla_ckv"], ckv], axis=1)
        kr_all = jnp.concatenate([cache["mla_krope"], kr], axis=1)
        kpos = jnp.arange(P + T, dtype=jnp.int32)
    kv = jnp.einsum("bsr,rn->bsn", ckv_all, lp["mla_wkv_b"]).reshape(B, -1, HC, NOPE + VDIM)
    k_nope, v_c = kv[..., :NOPE], kv[..., NOPE:]
    if cache is None:
        o_c = _query_blocks(lambda qp, qn, qr: _mla_core(qn, qr, k_nope, v_c, kr_all, qp, kpos),
                            pos, q_nope, q_rope)
        st_mla = (ckv[:, T - keep:], kr[:, T - keep:])
    else:
        o_c = _mla_core(q_nope, q_rope, k_nope, v_c, kr_all, pos, kpos)
        st_mla = (ckv, kr)
    o_c = o_c.reshape(B, T, WC)

    cin = d_c * d_h
    prev = jnp.zeros((B, CONV_W - 1, WD), cin.dtype) if cache is None else cache["conv"].astype(cin.dtype)
    xp = jnp.concatenate([prev, cin], axis=1)
    conv = xp[:, 0:T] * lp["conv_w"][0]
    for j in range(1, CONV_W):
        conv = conv + xp[:, j:j + T] * lp["conv_w"][j]
    o_d = d_b * conv
    st_conv = xp[:, T:]

    gates = jax.nn.sigmoid(m_g.reshape(B, T, N_BRANCH, D_MODEL))
    branches = (o_a * jax.nn.silu(a_g), o_b * jax.nn.silu(b_g), o_c * jax.nn.silu(c_g), o_d * jax.nn.silu(d_g))
    merged = gates[:, :, 0] * jnp.einsum("btw,wd->btd", branches[0], lp["w_branch"][0])
    for nbr in range(1, N_BRANCH):
        merged = merged + gates[:, :, nbr] * jnp.einsum("btw,wd->btd", branches[nbr], lp["w_branch"][nbr])
    x = x + jnp.einsum("btd,de->bte", merged, lp["w_out"])
    return x, st_fox + st_mla + (st_conv, st_gm)


def setup_inputs(seed: int = 0) -> dict:
    key = jax.random.key(seed)
    ks = jax.random.split(key, 24)
    nrm = jax.random.normal
    f32 = jnp.float32
    return {
        "x_prompt": nrm(ks[0], (BATCH, SEQ, D_MODEL), f32),
        "x_sample": nrm(ks[1], (DEC_BATCH, DEC_SEQ, D_MODEL), f32),
        "cache_fox_k": nrm(ks[2], (DEPTH, DEC_BATCH, PAST_LEN, HA, DHA), f32),
        "cache_fox_v": nrm(ks[3], (DEPTH, DEC_BATCH, PAST_LEN, HA, DHA), f32),
        "cache_fox_logf": jax.nn.log_sigmoid(FORGET_BIAS + nrm(ks[4], (DEPTH, DEC_BATCH, PAST_LEN, HA), f32)),
        "cache_mla_ckv": nrm(ks[5], (DEPTH, DEC_BATCH, PAST_LEN, KV_RANK), f32),
        "cache_mla_krope": nrm(ks[6], (DEPTH, DEC_BATCH, PAST_LEN, ROPE_DIM), f32),
        "state_conv": nrm(ks[7], (DEPTH, DEC_BATCH, CONV_W - 1, WD), f32),
        "norm_g": 1.0 + 0.02 * nrm(ks[8], (DEPTH, D_MODEL), f32),
        "w_in": nrm(ks[9], (DEPTH, D_MODEL, N_IN), f32) * D_MODEL ** -0.5,
        "fox_fb": FORGET_BIAS + 0.1 * nrm(ks[10], (DEPTH, HA), f32),
        "gm_ln_g": 1.0 + 0.02 * nrm(ks[11], (DEPTH, WB), f32),
        "gm_ln_b": 0.02 * nrm(ks[12], (DEPTH, WB), f32),
        "gm_ws": nrm(ks[13], (DEPTH, GB, GM_CHUNK, GM_CHUNK), f32) * GM_CHUNK ** -0.5,
        "gm_bs": 1.0 + 0.02 * nrm(ks[14], (DEPTH, GB, GM_CHUNK), f32),
        "mla_q_norm_g": 1.0 + 0.02 * nrm(ks[15], (DEPTH, Q_RANK), f32),
        "mla_wq_b": nrm(ks[16], (DEPTH, Q_RANK, HC * (NOPE + ROPE_DIM)), f32) * Q_RANK ** -0.5,
        "mla_kv_norm_g": 1.0 + 0.02 * nrm(ks[17], (DEPTH, KV_RANK), f32),
        "mla_wkv_b": nrm(ks[18], (DEPTH, KV_RANK, HC * (NOPE + VDIM)), f32) * KV_RANK ** -0.5,
        "conv_w": nrm(ks[19], (DEPTH, CONV_W, WD), f32) * CONV_W ** -0.5,
        "w_branch": nrm(ks[20], (DEPTH, N_BRANCH, WA, D_MODEL), f32) * WA ** -0.5,
        "w_out": nrm(ks[21], (DEPTH, D_MODEL, D_MODEL), f32) * D_MODEL ** -0.5,
        "final_norm_g": 1.0 + 0.02 * nrm(ks[22], (D_MODEL,), f32),
    }


def reference(x_prompt, x_sample, cache_fox_k, cache_fox_v, cache_fox_logf, cache_mla_ckv, cache_mla_krope,
              state_conv, norm_g, w_in, fox_fb, gm_ln_g, gm_ln_b, gm_ws, gm_bs, mla_q_norm_g, mla_wq_b,
              mla_kv_norm_g, mla_wkv_b, conv_w, w_branch, w_out, final_norm_g):
    pos_p = jnp.arange(x_prompt.shape[1], dtype=jnp.int32)
    pos_s = cache_fox_k.shape[2] + jnp.arange(x_sample.shape[1], dtype=jnp.int32)
    hp, hs = x_prompt, x_sample
    sp, ss = [], []
    for l in range(DEPTH):
        lp = {"norm_g": norm_g[l], "w_in": w_in[l], "fox_fb": fox_fb[l], "gm_ln_g": gm_ln_g[l],
              "gm_ln_b": gm_ln_b[l], "gm_ws": gm_ws[l], "gm_bs": gm_bs[l], "mla_q_norm_g": mla_q_norm_g[l],
              "mla_wq_b": mla_wq_b[l], "mla_kv_norm_g": mla_kv_norm_g[l], "mla_wkv_b": mla_wkv_b[l],
              "conv_w": conv_w[l], "w_branch": w_branch[l], "w_out": w_out[l]}
        cache = {"fox_k": cache_fox_k[l], "fox_v": cache_fox_v[l], "fox_logf": cache_fox_logf[l],
                 "mla_ckv": cache_mla_ckv[l], "mla_krope": cache_mla_krope[l], "conv": state_conv[l]}
        hp, st_p = _layer(hp, pos_p, lp, None)
        hs, st_s = _layer(hs, pos_s, lp, cache)
        sp.append(st_p)
        ss.append(st_s)
    y_prompt = _rmsnorm(hp, final_norm_g)
    y_sample = _rmsnorm(hs, final_norm_g)
    p_fox_k, p_fox_v, p_fox_logf, p_mla_ckv, p_mla_krope, p_conv, p_gmlp_v = [
        jnp.stack([s[i] for s in sp]) for i in range(7)]
    s_fox_k, s_fox_v, s_fox_logf, s_mla_ckv, s_mla_krope, s_conv, s_gmlp_v = [
        jnp.stack([s[i] for s in ss]) for i in range(7)]
    return (y_prompt, y_sample, p_fox_k, p_fox_v, p_fox_logf, p_mla_ckv, p_mla_krope, p_conv, p_gmlp_v,
            s_fox_k, s_fox_v, s_fox_logf, s_mla_ckv, s_mla_krope, s_conv, s_gmlp_v)
```

```python
import numpy as np
import ml_dtypes
from contextlib import ExitStack
import concourse.bass as bass
import concourse.mybir as mybir
from concourse.bass_utils import run_bass_kernel_spmd

F32 = mybir.dt.float32
BF16 = mybir.dt.bfloat16
AF = mybir.ActivationFunctionType
ALU = mybir.AluOpType

P = 128
DM = 2048
KC = 16
L = 2
TOK = 512
NS = 32
PAST = 1024
EPS = 1e-6
NA = 20
NB = 139
TROWS = 528
OFF = dict(a_q=0, a_k=512, a_v=1024, a_f=1536, a_g=1540, b_u=2052, b_v=2564, b_g=3076, c_q=3588,
           c_kv=3972, c_kr=4228, c_g=4292, d_b=4804, d_c=5316, d_h=5828, d_g=6340, m_g=6852)
NEG_KILL = -30000.0


class Op:
    __slots__ = ("eng", "fn", "deps", "sig", "val", "dkey", "inc")


class Sched:
    ENGS = ("pe", "act", "dve", "pool", "sp")

    def __init__(self):
        self.ops = {e: [] for e in self.ENGS}
        self.lastw = {}
        self.readers = {}
        self.dcnt = {}
        self.pend = {e: None for e in self.ENGS}
        self.dlast = {}
        import os
        self.limit = int(os.environ.get("KOPS", "1000000000"))

    def add(self, eng, fn, reads=(), writes=(), dkey=None, inc=16, extra=(), chain=True):
        self.nops = getattr(self, "nops", 0) + 1
        if self.nops > self.limit:
            return None
        extra = [x for x in extra if x is not None]
        op = Op()
        op.eng, op.fn, op.sig, op.dkey, op.val, op.inc = eng, fn, False, dkey, None, inc
        deps = set(extra)
        if dkey is not None and chain and dkey in self.dlast:
            deps.add(self.dlast[dkey])
        for k in reads:
            w = self.lastw.get(k)
            if w is not None:
                deps.add(w)
        for k in writes:
            w = self.lastw.get(k)
            if w is not None:
                deps.add(w)
            rd = self.readers.get(k)
            if rd:
                deps.update(rd.values())
        for k in reads:
            self.readers.setdefault(k, {})[dkey if dkey is not None else eng] = op
        for k in writes:
            self.lastw[k] = op
            self.readers[k] = {}
        if self.pend[eng] is not None:
            deps.update(self.pend[eng])
            self.pend[eng] = None
        deps.discard(op)
        if eng == "pe":
            deps = {d for d in deps if not (d.eng == "pe" and d.dkey is None)}
        op.deps = deps
        if dkey is not None:
            v = self.dcnt.get(dkey, 0) + inc
            self.dcnt[dkey] = v
            op.val = v
            self.dlast[dkey] = op
        for d in deps:
            d.sig = True
        self.ops[eng].append(op)
        return op

    def barrier(self):
        s = set()
        for e in self.ENGS:
            for op in reversed(self.ops[e]):
                if op.dkey is None:
                    s.add(op)
                    break
        s.update(self.dlast.values())
        for op in s:
            op.sig = True
        for e in self.ENGS:
            self.pend[e] = set(s) if self.pend[e] is None else (self.pend[e] | s)

    def finalize(self):
        for e in self.ENGS:
            c = 0
            for op in self.ops[e]:
                if op.dkey is None and op.sig:
                    c += 1
                    op.val = c

    def emit(self, e, eng, semof):
        known = {}
        for op in self.ops[e]:
            need = {}
            for d in op.deps:
                key = ("d", d.dkey) if d.dkey is not None else ("e", d.eng)
                if d.val > need.get(key, 0):
                    need[key] = d.val
            for key, v in need.items():
                if known.get(key, 0) >= v:
                    continue
                known[key] = v
                eng.wait_ge(semof(key), v)
            ins = op.fn(eng)
            if op.dkey is not None:
                ins.then_inc(semof(("d", op.dkey)), op.inc)
            elif op.sig:
                ins.then_inc(semof(("e", e)), 1)
        if e == "sp":
            for dk, v in self.dcnt.items():
                if known.get(("d", dk), 0) < v:
                    eng.wait_ge(semof(("d", dk)), v)


def build_program(NG):
    CH = NG * 16
    MC = NG * 16 + NG * 8
    ROWS = NG * TROWS
    nc = bass.Bass("TRN2", target_bir_lowering=False)
    S = Sched()

    def DI(name, shape, dt=F32):
        return nc.dram_tensor(name, list(shape), dt, kind="ExternalInput")

    def DO(name, shape, dt=F32):
        return nc.dram_tensor(name, list(shape), dt, kind="ExternalOutput")

    xp = DI("xp", [NG * P, KC * TOK])
    xs = DI("xs", [P, KC * NS])
    waf = DI("waf", [L * NA * P, 2048])
    wbf = DI("wbf", [L * NB * P, 2048])
    wqf = DI("wqf", [L * P, 3 * 1024])
    wkvf = DI("wkvf", [L * P, 2 * 1024])
    c_fkT = DI("c_fkT", [L * 2 * P, 4 * PAST])
    c_fv = DI("c_fv", [L * 2 * P, 8 * 512])
    c_flf = DI("c_flf", [L * 2 * P, 8 * 4])
    c_ckvT = DI("c_ckvT", [L * 2 * P, 2 * PAST])
    c_krT = DI("c_krT", [L * 2 * 64, PAST])
    c_conv = DI("c_conv", [L * 2 * P, 8])
    p_ng = DI("p_ng", [L * P, KC])
    p_fb = DI("p_fb", [L * P, 16])
    p_lng = DI("p_lng", [L * P, 512])
    p_lnb = DI("p_lnb", [L * P, 512])
    p_wmT = DI("p_wmT", [L * P, 512])
    p_bs = DI("p_bs", [L * P, 512])
    p_qg = DI("p_qg", [L * P, 3])
    p_kvg = DI("p_kvg", [L * P, 2])
    p_cw = DI("p_cw", [L * P, 12])
    p_fg = DI("p_fg", [P, KC])
    t_cos = DI("t_cos", [NG * 64, TOK])
    t_sin = DI("t_sin", [NG * 64, TOK])
    t_cos_s = DI("t_cos_s", [64, NS])
    t_sin_s = DI("t_sin_s", [64, NS])
    k_mf = DI("k_mf", [P, 2048], BF16)
    k_mm = DI("k_mm", [P, 2048], BF16)
    k_U = DI("k_U", [P, P])
    k_sc = DI("k_sc", [P, 16])
    yp = DO("yp", [NG * P, KC * TOK])
    ys = DO("ys", [P, KC * NS])
    o_fkT = DO("o_fkT", [L * P, 2048])
    o_fv = DO("o_fv", [L * P, 2048])
    o_flf = DO("o_flf", [L * P, 16])
    o_ckvT = DO("o_ckvT", [L * P, 1024])
    o_krT = DO("o_krT", [L * 64, 512])
    o_halo = DO("o_halo", [L * P, NG * 8])
    o_gvn = DO("o_gvn", [L * P, 512])
    os_fkT = DO("os_fkT", [L * P, 4 * NS])
    os_fv = DO("os_fv", [L * 2 * 16, 512])
    os_flf = DO("os_flf", [L * 2 * 16, 4])
    os_ckvT = DO("os_ckvT", [L * P, 2 * NS])
    os_krT = DO("os_krT", [L * 64, NS])
    os_conv = DO("os_conv", [L * P, 16])
    os_gvn = DO("os_gvn", [L * 2 * 16, 512])
    wab = nc.dram_tensor("wab", [L * NA * P, 2048], BF16)
    wbb = nc.dram_tensor("wbb", [L * NB * P, 2048], BF16)
    x1p = nc.dram_tensor("x1p", [NG * P, KC * TOK], F32)
    x2p = nc.dram_tensor("x2p", [NG * P, KC * TOK], F32)
    SECR = [128, 128, 128, 128, 16]
    gin = [[[nc.dram_tensor(f"gin{l}_{g}_{s_}", [SECR[s_], 2048], BF16) for s_ in range(5)] for g in range(NG)] for l in range(L)]
    gout = [[[nc.dram_tensor(f"gout{l}_{g}_{s_}", [4 * SECR[s_], 2048], BF16) for s_ in range(5)] for g in range(NG)] for l in range(L)]
    min_ = [nc.dram_tensor(f"min{l}", [P, MC], F32) for l in range(L)]
    mout = [nc.dram_tensor(f"mout{l}", [4 * P, MC], F32) for l in range(L)]

    es = ExitStack()
    with es:
        ARENA = 103 * 1024
        arena = es.enter_context(nc.sbuf_tensor("arena", [P, ARENA], BF16))
        psb = [es.enter_context(nc.psum_tensor(f"ps{i}", [P, 512], F32)) for i in range(8)]
        sems = {}

        def semof(key):
            if key not in sems:
                sems[key] = es.enter_context(nc.semaphore(f"s{len(sems)}"))
            return sems[key]

        class Alloc:
            def __init__(self):
                self.off = 0

            def h(self, *shape, parts=P):
                n = int(np.prod(shape))
                ap = arena[0:parts, self.off:self.off + n]
                self.off += (n + 15) // 16 * 16
                assert self.off <= ARENA, f"arena overflow {self.off}"
                return self._shape(ap, shape)

            def f(self, *shape, parts=P):
                n = int(np.prod(shape))
                ap = arena[0:parts, self.off:self.off + 2 * n].bitcast(F32)
                self.off += (2 * n + 15) // 16 * 16
                assert self.off <= ARENA, f"arena overflow {self.off}"
                return self._shape(ap, shape)

            @staticmethod
            def _shape(ap, shape):
                if len(shape) == 1:
                    return ap
                if len(shape) == 2:
                    return ap.rearrange("p (a b) -> p a b", b=shape[1])
                if len(shape) == 3:
                    return ap.rearrange("p (a b c) -> p a b c", b=shape[1], c=shape[2])
                raise ValueError

        A = Alloc()

        onesH = A.h(P)
        ones32 = A.f(P)
        U32 = A.f(P)
        UH = A.h(P)
        sc = A.f(16)
        fgam = A.f(KC)
        onesCH = A.f(max(CH, 16))
        ksT = A.h(4, NS)
        knS_s = A.h(4, NS)
        krS_s = A.h(NS)
        ckS_s = A.h(2, NS)
        vS_s = A.h(2, 512)
        vcS_s = A.h(2, 512)
        lf_s = A.f(2, 4)
        xs_cur = A.f(KC, NS)
        ng = A.f(KC)
        fbb = A.f(16)
        lng = A.f(512)
        lnb = A.f(512)
        wmT = A.h(4, P)
        bsb = A.f(4, P)
        qg = A.f(3)
        kvg = A.f(2)
        cw = A.f(4, 3)
        wq = A.h(3, 1024)
        wkv = A.h(2, 1024)
        mf = A.h(4, 512)
        mm_ = A.h(4, 512)
        PERSIST = A.off

        _rot = [0]

        def rot():
            b = _rot[0] % 4
            _rot[0] += 1
            return b

        def dma(eng, out, in_, reads, writes, dkey, extra=(), cast=False, chain=True):
            if cast:
                fn = lambda e, o=out, i=in_: e.dma_start(out=o, in_=i, max_dma_last_dim=4096)
            else:
                fn = lambda e, o=out, i=in_: e.dma_start(out=o, in_=i)
            return S.add(eng, fn, reads, writes, dkey=dkey, extra=extra, chain=chain)

        def mm(bank, out, pairs, reads):
            def fn(pe, out=out, pairs=pairs):
                n = len(pairs)
                ins = None
                for i, (l_, r_) in enumerate(pairs):
                    ins = pe.matmul(out, l_, r_, start=(i == 0), stop=(i == n - 1))
                return ins
            return S.add("pe", fn, reads, [("ps", bank)])

        def mm_acc(bank, out, pairs, reads, first, last):
            def fn(pe, out=out, pairs=pairs, first=first, last=last):
                n = len(pairs)
                ins = None
                for i, (l_, r_) in enumerate(pairs):
                    ins = pe.matmul(out, l_, r_, start=(first and i == 0), stop=(last and i == n - 1))
                return ins
            return S.add("pe", fn, reads, [("ps", bank)])

        def act(out, in_, func, reads, writes, bias=0.0, scale=1.0):
            return S.add("act", lambda e, o=out, i=in_, f=func, b=bias, s=scale: e.activation(o, i, f, bias=b, scale=s),
                         reads, writes)

        def tt(out, in0, in1, op, reads, writes, eng="dve"):
            return S.add(eng, lambda e, o=out, a=in0, b=in1, op=op: e.tensor_tensor(o, a, b, op), reads, writes)

        def ts(out, in0, s1, s2, op0, op1, reads, writes, eng="dve"):
            if op1 is None:
                return S.add(eng, lambda e, o=out, a=in0, s1=s1, op0=op0: e.tensor_scalar(o, a, s1, None, op0), reads, writes)
            return S.add(eng, lambda e, o=out, a=in0, s1=s1, s2=s2, op0=op0, op1=op1: e.tensor_scalar(o, a, s1, s2, op0, op1),
                         reads, writes)

        def stt(out, in0, scal, in1, op0, op1, reads, writes):
            return S.add("dve", lambda e, o=out, a=in0, s=scal, b=in1, op0=op0, op1=op1:
                         e.scalar_tensor_tensor(o, a, s, b, op0, op1), reads, writes)

        def cp(out, in_, reads, writes, eng="dve"):
            return S.add(eng, lambda e, o=out, i=in_: e.tensor_copy(o, i), reads, writes)

        def multi(eng, fns, reads, writes):
            def fn(e, fns=fns):
                ins = None
                for f in fns:
                    ins = f(e)
                return ins
            return S.add(eng, fn, reads, writes)

        conv_ops = {}
        bgq = []
        cvn = [0]

        def queue_convert(src, dst, r0, r1, key):
            r = r0
            while r < r1:
                n = min(1024, r1 - r)
                bgq.append((src, dst, r, n, key))
                r += n

        def bg_tick(n=1000):
            while bgq and n > 0:
                src, dst, r, cnt, key = bgq.pop(0)
                op = dma("pool", dst[r:r + cnt, :], src[r:r + cnt, :], [], [], dkey=("cv", cvn[0] % 4))
                cvn[0] += 1
                conv_ops.setdefault(key, []).append(op)
                n -= 1

        def need_conv(key):
            while any(b[4] == key for b in bgq):
                bg_tick(1)
            return conv_ops.get(key, [])

        for l in range(L):
            queue_convert(waf, wab, l * NA * P, (l + 1) * NA * P, ("wab", l))
            queue_convert(wbf, wbb, l * NB * P, (l + 1) * NB * P, ("wbb", l))

        S.add("dve", lambda e: e.memset(ones32, 1.0), [], ["ones32"])
        S.add("dve", lambda e: e.memset(onesH, 1.0), [], ["onesH"])
        S.add("dve", lambda e: e.memset(onesCH, 1.0), [], ["onesCH"])
        dma("sp", U32, k_U[:, :], [], ["U32"], dkey="c")
        cp(UH, U32, ["U32"], ["UH"])
        dma("sp", sc, k_sc[:, :], [], ["sc"], dkey="c")
        dma("sp", fgam, p_fg[:, :], [], ["fgam"], dkey="c")
        dma("sp", mf, k_mf[:, :].rearrange("p (a b) -> p a b", b=512), [], ["mf"], dkey="c")
        dma("sp", mm_, k_mm[:, :].rearrange("p (a b) -> p a b", b=512), [], ["mm"], dkey="c")
        selA = [sc[:, r:r + 1] for r in range(4)]
        kill = [sc[:, 4 + r:5 + r] for r in range(4)]
        sel = [sc[:, 8 + r:9 + r] for r in range(4)]
        selP = [sc[:, 12 + r:13 + r] for r in range(4)]
        LP = "lp"
        CONST = ["onesH", "ones32", "U32", "UH", "sc", "fgam", "mf", "mm", "onesCH"]

        def load_layer_params(l):
            rs = slice(l * P, (l + 1) * P)
            k = LP
            dma("sp", ng, p_ng[rs, :], [], [k], dkey="lp")
            dma("sp", fbb, p_fb[rs, :], [], [k], dkey="lp")
            dma("sp", lng, p_lng[rs, :], [], [k], dkey="lp")
            dma("sp", lnb, p_lnb[rs, :], [], [k], dkey="lp")
            dma("sp", bsb, p_bs[rs, :].rearrange("p (a b) -> p a b", b=P), [], [k], dkey="lp")
            dma("sp", qg, p_qg[rs, :], [], [k], dkey="lp")
            dma("sp", kvg, p_kvg[rs, :], [], [k], dkey="lp")
            dma("sp", cw, p_cw[rs, :].rearrange("p (a b) -> p a b", b=3), [], [k], dkey="lp")
            dma("pool", wq, wqf[rs, :].rearrange("p (a b) -> p a b", b=1024), [], [k], dkey="lpc", cast=True)
            dma("pool", wkv, wkvf[rs, :].rearrange("p (a b) -> p a b", b=1024), [], [k], dkey="lpc", cast=True)
            dma("pool", wmT, p_wmT[rs, :].rearrange("p (a b) -> p a b", b=P), [], [k], dkey="lpc", cast=True)
            for g_ in range(4):
                tt(wmT[:, g_, :], wmT[:, g_, :], UH, ALU.mult, [k, "UH"], [k])

        def ssq_to_rstd(bank, N, dim, rstd, tmpl):
            act(tmpl[:, 0:N], psb[bank][:, 0:N], AF.Ln, [("ps", bank)], ["tmpl"], bias=EPS, scale=1.0 / dim)
            act(rstd[:, 0:N], tmpl[:, 0:N], AF.Exp, ["tmpl"], ["rstd"], scale=-0.5)

        def rms_sbuf(src, nkc, N, dim, sqb, rstd, tmpl, skey):
            bank = rot()
            ngrp = (nkc + 1) // 2
            for gi in range(ngrp):
                k0, k1 = gi * 2, min(nkc, gi * 2 + 2)
                sk = ("sq", gi % 2)
                act(sqb[gi % 2][:, 0:k1 - k0, 0:N], src[:, k0:k1, 0:N], AF.Square, [skey], [sk])
                mm_acc(bank, psb[bank][:, 0:N], [(onesH, sqb[gi % 2][:, i, 0:N]) for i in range(k1 - k0)], [sk, "onesH"],
                       first=(gi == 0), last=(gi == ngrp - 1))
            ssq_to_rstd(bank, N, dim, rstd, tmpl)

        def rms_stream(src_rows, xr, sqb, rstd, tmpl, rd_extra):
            bank = rot()
            for i in range(8):
                xk = ("xr", i % 2)
                dma("sp", xr[i % 2], src_rows[:, i * 2 * TOK:(i + 1) * 2 * TOK].rearrange("p (a b) -> p a b", b=TOK), [], [xk],
                    dkey=xk, extra=rd_extra)
                sk = ("sq", i % 2)
                act(sqb[i % 2], xr[i % 2], AF.Square, [xk], [sk])
                mm_acc(bank, psb[bank][:, :], [(onesH, sqb[i % 2][:, k, :]) for k in range(2)], [sk, "onesH"], first=(i == 0), last=(i == 7))
            ssq_to_rstd(bank, TOK, DM, rstd, tmpl)

        def scale_stream(src_rows, xr, rstd, gam, dst_fn, rd_extra, dkeys, wkeys):
            for i in range(8):
                xk = ("xr", i % 2)
                dma("sp", xr[i % 2], src_rows[:, i * 2 * TOK:(i + 1) * 2 * TOK].rearrange("p (a b) -> p a b", b=TOK), [], [xk],
                    dkey=xk, extra=rd_extra)
                multi("dve", [lambda e, i=i, k=k: e.scalar_tensor_tensor(dst_fn(2 * i + k), xr[i % 2][:, k, :], gam[:, 2 * i + k:2 * i + k + 1],
                                                                         rstd[:, :], ALU.mult, ALU.mult) for k in range(2)],
                      [xk, "rstd"] + dkeys, wkeys(i))

        def make_hT_prompt(src_rows, hT, xr, sqb, rstd, tmpl, rd_extra):
            rms_stream(src_rows, xr, sqb, rstd, tmpl, rd_extra)
            scale_stream(src_rows, xr, rstd, ng, lambda kc: hT[:, kc, :], rd_extra, [LP], lambda i: ["hT"])

        def make_hT_sample(hT, sqb, rstd, tmpl):
            rms_sbuf(xs_cur, KC, NS, DM, sqb, rstd, tmpl, "xs_cur")
            multi("dve", [lambda e, kc=kc: e.scalar_tensor_tensor(hT[:, kc, 0:NS], xs_cur[:, kc, :], ng[:, kc:kc + 1], rstd[:, 0:NS],
                                                                  ALU.mult, ALU.mult) for kc in range(KC)],
                  ["xs_cur", "rstd", LP], ["hT"])

        def proj_fm(w, src, N, reads, M=P, c0=0, nkc=KC):
            bank = rot()
            out = psb[bank][0:M, 0:N]
            mm(bank, out, [(w[:, kc * P + c0: kc * P + c0 + M], src[:, kc, 0:N]) for kc in range(nkc)], reads)
            return bank, out

        x_stores = {}
        AGR = [[0, 1, 2, 3], [4, 5, 6, 7]]
        import os as _os2
        NOCC = bool(_os2.environ.get("KNOCC", ""))

        def allgather(src_t, dst_t, rows, wkey, dkey, extra):
            if NOCC:
                for r_ in range(4):
                    S.add("pool", lambda e, r_=r_: e.dma_start(out=dst_t[r_ * rows:(r_ + 1) * rows, :], in_=src_t[:, :]),
                          [], [wkey], dkey=dkey, inc=16, extra=extra)
            else:
                S.add("pool", lambda e: e.collective_compute("AllGather", ALU.bypass, replica_groups=AGR,
                                                             ins=[src_t.ap().opt()], outs=[dst_t.ap().opt()]),
                      [], [wkey], dkey=dkey, inc=1, extra=extra)

        def gstore(l, g, sec, view, src_ap, skey):
            st = dma("pool", view(gin[l][g][sec]), src_ap, [skey], [], dkey=skey)
            allgather(gin[l][g][sec], gout[l][g][sec], SECR[sec], ("gout", l, g, sec), ("ccg", l, sec), [st])

        def pass_A(l, xsrc_p, xsrc_name):
            A.off = PERSIST
            S.barrier()
            WA = A.h(12, 2048)
            hT = A.h(KC, TOK)
            sqb = [A.h(2, TOK), A.h(2, TOK)]
            xr = [A.f(2, TOK), A.f(2, TOK)]
            rstd = A.f(TOK)
            tmpl = A.f(TOK)
            ktS = [A.h(4, TOK), A.h(4, TOK)]
            vS = [A.h(4, 512), A.h(4, 512)]
            knS = [A.h(4, TOK), A.h(4, TOK)]
            vcS = [A.h(4, 512), A.h(4, 512)]
            krS = [A.h(TOK), A.h(TOK)]
            ckS = A.h(2, TOK)
            ck32 = A.f(2, TOK)
            ckF = A.f(2, TOK)
            stF = A.f(4, 512)
            krF = A.f(TOK)
            kt1 = A.f(TOK)
            kt2 = A.f(TOK)
            cosb = A.f(TOK)
            sinb = A.f(TOK)
            lfall = A.f(NG, 16)
            lft = A.f(16)
            hh = A.h(KC, 2 * NG)
            halo = A.f(4, 2 * NG)
            ringA = [A.h(2048), A.h(2048)]
            htmp = A.f(2 * NG)

            load_layer_params(l)
            cvA = need_conv(("wab", l))
            dma("sp", WA, wab[l * NA * P:(l * NA + 12) * P, :].rearrange("(c p) n -> p c n", p=P), [], ["WA"], dkey="WA", extra=cvA)

            tiles = [("p", g) for g in range(NG)] + [("s", 0)]
            for ti, (kind, g) in enumerate(tiles):
                N = TOK if kind == "p" else NS
                lastp = (kind == "p" and g == NG - 1)
                sb = ti % 2
                if l == 0:
                    for _ in range(2):
                        if bgq and bgq[0][4][1] == 0:
                            bg_tick(1)
                if kind == "p":
                    rows = xsrc_p[g * P:(g + 1) * P, :]
                    make_hT_prompt(rows, hT, xr, sqb, rstd, tmpl, x_stores.get((xsrc_name, g), []))
                    S.add("sp", lambda e, g=g: e.dma_start(out=cosb[0:64, :], in_=t_cos[g * 64:(g + 1) * 64, :]), [], ["cos"], dkey="cos")
                    S.add("sp", lambda e, g=g: e.dma_start(out=sinb[0:64, :], in_=t_sin[g * 64:(g + 1) * 64, :]), [], ["sin"], dkey="sin")
                else:
                    if l == 0:
                        dma("sp", xs_cur, xs[:, :].rearrange("p (a b) -> p a b", b=NS), [], ["xs_cur"], dkey="xs")
                    S.add("sp", lambda e: e.dma_start(out=cosb[0:64, 0:NS], in_=t_cos_s[:, :]), [], ["cos"], dkey="cos")
                    S.add("sp", lambda e: e.dma_start(out=sinb[0:64, 0:NS], in_=t_sin_s[:, :]), [], ["sin"], dkey="sin")
                    make_hT_sample(hT, sqb, rstd, tmpl)
                RW = ["hT", "WA"]
                for h in range(4):
                    bank, ps = proj_fm(WA[:, h, :], hT, N, RW)
                    if kind == "p":
                        if lastp:
                            act(stF[:, h, :], ps, AF.Identity, [("ps", bank)], ["stF"])
                            cp(ktS[sb][:, h, :], stF[:, h, :], ["stF"], [("ktS", sb)])
                        else:
                            act(ktS[sb][:, h, :], ps, AF.Identity, [("ps", bank)], [("ktS", sb)])
                    else:
                        act(stF[:, h, 0:NS], ps, AF.Identity, [("ps", bank)], ["stF"])
                        cp(ksT[:, h, :], stF[:, h, 0:NS], ["stF"], ["ksT"])
                if kind == "p":
                    gstore(l, g, 0, lambda t: t[:, :].rearrange("p (a b) -> p a b", b=TOK), ktS[sb], ("ktS", sb))
                    if lastp:
                        dma("pool", o_fkT[l * P:(l + 1) * P, :].rearrange("p (a b) -> p a b", b=TOK), stF, ["stF"], [], dkey="stF")
                else:
                    dma("pool", os_fkT[l * P:(l + 1) * P, :].rearrange("p (a b) -> p a b", b=NS), stF[:, :, 0:NS], ["stF"], [], dkey="stF")
                if kind == "p":
                    for sub in range(4):
                        bank = rot()
                        for vc in range(4):
                            mm(bank, psb[bank][:, vc * P:(vc + 1) * P],
                               [(hT[:, kc, sub * P:(sub + 1) * P], WA[:, 4 + vc, kc * P:(kc + 1) * P]) for kc in range(KC)], RW)
                        if lastp:
                            act(stF[:, sub, :], psb[bank][:, :], AF.Identity, [("ps", bank)], ["stF"])
                            cp(vS[sb][:, sub, :], stF[:, sub, :], ["stF"], [("vS", sb)])
                        else:
                            act(vS[sb][:, sub, :], psb[bank][:, :], AF.Identity, [("ps", bank)], [("vS", sb)])
                    gstore(l, g, 1, lambda t: t[:, :].rearrange("p (a b) -> p a b", b=512), vS[sb], ("vS", sb))
                    if lastp:
                        dma("pool", o_fv[l * P:(l + 1) * P, :].rearrange("p (a b) -> p a b", b=512), stF, ["stF"], [], dkey="stF")
                else:
                    for sq_ in range(2):
                        bank = rot()
                        for vc in range(4):
                            mm(bank, psb[bank][0:16, vc * P:(vc + 1) * P],
                               [(hT[:, kc, sq_ * 16:(sq_ + 1) * 16], WA[:, 4 + vc, kc * P:(kc + 1) * P]) for kc in range(KC)], RW)
                        act(stF[0:16, sq_, :], psb[bank][0:16, :], AF.Identity, [("ps", bank)], ["stF"])
                        cp(vS_s[0:16, sq_, :], stF[0:16, sq_, :], ["stF"], ["vS_s"])
                    for sq_ in range(2):
                        dma("pool", os_fv[(l * 2 + sq_) * 16:(l * 2 + sq_ + 1) * 16, :], stF[0:16, sq_, :], ["stF"], [], dkey="stF")
                bank = rot()
                if kind == "p":
                    for sub in range(4):
                        mm(bank, psb[bank][:, sub * 4:(sub + 1) * 4],
                           [(hT[:, kc, sub * P:(sub + 1) * P], WA[:, 11, kc * P:kc * P + 4]) for kc in range(KC)], RW)
                    tt(lft, psb[bank][:, 0:16], fbb, ALU.add, [("ps", bank), LP], ["lft"])
                    act(lft, lft, AF.Exp, ["lft"], ["lft"], scale=-1.0)
                    act(lft, lft, AF.Ln, ["lft"], ["lft"], bias=1.0)
                    ts(lfall[:, g, :], lft, -1.0, None, ALU.mult, None, ["lft"], ["lfall"])
                    if lastp:
                        dma("pool", o_flf[l * P:(l + 1) * P, :], lfall[:, g, :], ["lfall"], [], dkey="lfo")
                else:
                    for sq_ in range(2):
                        mm(bank, psb[bank][0:16, sq_ * 4:(sq_ + 1) * 4],
                           [(hT[:, kc, sq_ * 16:(sq_ + 1) * 16], WA[:, 11, kc * P:kc * P + 4]) for kc in range(KC)], RW)
                    tt(lft[0:16, 0:8], psb[bank][0:16, 0:8], fbb[0:16, 0:8], ALU.add, [("ps", bank), LP], ["lft"])
                    act(lft[0:16, 0:8], lft[0:16, 0:8], AF.Exp, ["lft"], ["lft"], scale=-1.0)
                    act(lft[0:16, 0:8], lft[0:16, 0:8], AF.Ln, ["lft"], ["lft"], bias=1.0)
                    ts(lf_s[0:16, :, :].rearrange("p a b -> p (a b)"), lft[0:16, 0:8], -1.0, None, ALU.mult, None, ["lft"], ["lf_s"])
                    for sq_ in range(2):
                        dma("pool", os_flf[(l * 2 + sq_) * 16:(l * 2 + sq_ + 1) * 16, :], lf_s[0:16, sq_, :], ["lf_s"], [], dkey="lfo")
                for i in range(2):
                    bank, ps = proj_fm(WA[:, 8 + i, :], hT, N, RW)
                    cp(ck32[:, i, 0:N], ps, [("ps", bank)], ["ck32"])
                rms_sbuf(ck32, 2, N, 256, sqb, rstd, tmpl, "ck32")
                ckd = ckS if kind == "p" else ckS_s
                ckk = "ckS" if kind == "p" else "ckS_s"
                multi("dve", [lambda e, i=i, ckd=ckd, N=N: e.scalar_tensor_tensor(ckd[:, i, 0:N], ck32[:, i, 0:N], kvg[:, i:i + 1], rstd[:, 0:N],
                                                                 ALU.mult, ALU.mult) for i in range(2)],
                      ["ck32", "rstd", LP], [ckk])
                if lastp or kind == "s":
                    multi("dve", [lambda e, i=i, N=N: e.scalar_tensor_tensor(ckF[:, i, 0:N], ck32[:, i, 0:N], kvg[:, i:i + 1], rstd[:, 0:N],
                                                                     ALU.mult, ALU.mult) for i in range(2)],
                          ["ck32", "rstd", LP], ["ckF"])
                    if lastp:
                        dma("pool", o_ckvT[l * P:(l + 1) * P, :].rearrange("p (a b) -> p a b", b=TOK), ckF, ["ckF"], [], dkey="ckF")
                    else:
                        dma("pool", os_ckvT[l * P:(l + 1) * P, :].rearrange("p (a b) -> p a b", b=NS), ckF[:, :, 0:NS], ["ckF"], [], dkey="ckF")
                bA, psA = proj_fm(WA[:, 10, :], hT, N, RW, M=64, c0=0)
                bB, psB = proj_fm(WA[:, 10, :], hT, N, RW, M=64, c0=64)
                tt(kt1[0:64, 0:N], psA, cosb[0:64, 0:N], ALU.mult, [("ps", bA), "cos"], ["kt1"])
                tt(kt2[0:64, 0:N], psB, sinb[0:64, 0:N], ALU.mult, [("ps", bB), "sin"], ["kt2"])
                tt(krF[0:64, 0:N], kt1[0:64, 0:N], kt2[0:64, 0:N], ALU.add, ["kt1", "kt2"], ["krF"])
                if kind == "p":
                    cp(krS[sb][0:64, :], krF[0:64, :], ["krF"], [("krS", sb)])
                    gstore(l, g, 4, lambda t: t[:, :].rearrange("a (b c) -> (a b) c", c=TOK), krS[sb][0:64, :], ("krS", sb))
                    if lastp:
                        dma("pool", o_krT[l * 64:(l + 1) * 64, :], krF[0:64, :], ["krF"], [], dkey="krF")
                else:
                    cp(krS_s[0:64, :], krF[0:64, 0:NS], ["krF"], ["krS_s"])
                    dma("pool", os_krT[l * 64:(l + 1) * 64, :], krF[0:64, 0:NS], ["krF"], [], dkey="krF")
                for h in range(4):
                    bank = rot()
                    ps = psb[bank][:, 0:N]
                    mm(bank, ps, [(wkv[:, i, h * P:(h + 1) * P], ckd[:, i, 0:N]) for i in range(2)], [ckk, LP])
                    if kind == "p":
                        act(knS[sb][:, h, :], ps, AF.Identity, [("ps", bank)], [("knS", sb)])
                    else:
                        act(knS_s[:, h, :], ps, AF.Identity, [("ps", bank)], ["knS_s"])
                if kind == "p":
                    gstore(l, g, 2, lambda t: t[:, :].rearrange("p (a b) -> p a b", b=TOK), knS[sb], ("knS", sb))
                    for sub in range(4):
                        bank = rot()
                        mm(bank, psb[bank][:, :], [(ckS[:, i, sub * P:(sub + 1) * P], wkv[:, i, 512:1024]) for i in range(2)], ["ckS", LP])
                        act(vcS[sb][:, sub, :], psb[bank][:, :], AF.Identity, [("ps", bank)], [("vcS", sb)])
                    gstore(l, g, 3, lambda t: t[:, :].rearrange("p (a b) -> p a b", b=512), vcS[sb], ("vcS", sb))
                    cp(hh[:, :, 2 * g:2 * g + 2], hT[:, :, TOK - 2:TOK], ["hT"], ["hh"])
                else:
                    for sq_ in range(2):
                        bank = rot()
                        mm(bank, psb[bank][0:16, :], [(ckS_s[:, i, sq_ * 16:(sq_ + 1) * 16], wkv[:, i, 512:1024]) for i in range(2)], ["ckS_s", LP])
                        act(vcS_s[0:16, sq_, :], psb[bank][0:16, :], AF.Identity, [("ps", bank)], ["vcS_s"])
            for ch in range(4):
                banks = []
                for which in range(2):
                    si = (ch * 2 + which) % 2
                    sk = ("ringA", si)
                    row = (l * NA + 12 + which * 4 + ch) * P
                    dma("sp", ringA[si], wab[row:row + P, :], [], [sk], dkey=sk, extra=cvA)
                    bank, ps = proj_fm(ringA[si], hh, 2 * NG, [sk, "hh"])
                    banks.append((bank, ps))
                cp(htmp[:, :], banks[0][1], [("ps", banks[0][0])], ["htmp"])
                tt(halo[:, ch, :], banks[1][1], htmp[:, :], ALU.mult, [("ps", banks[1][0]), "htmp"], ["halo"])
            m0 = dma("pool", min_[l][:, 0:NG * 16].rearrange("p (a b) -> p a b", b=16), lfall, ["lfall"], [], dkey="min0")
            m1 = dma("pool", min_[l][:, NG * 16:MC].rearrange("p (a b) -> p a b", b=2 * NG), halo, ["halo"], [], dkey="min1")
            dma("pool", o_halo[l * P:(l + 1) * P, :].rearrange("p (a b) -> p a b", b=2 * NG), halo, ["halo"], [], dkey="min2")
            allgather(min_[l], mout[l], P, ("mout", l), ("ccm", l), [m0, m1])

        def pass_B(l, xsrc_p, xsrc_name, xdst_p, xdst_name):
            A.off = PERSIST
            S.barrier()
            NSLOT = 6
            ring = [A.h(2048) for _ in range(NSLOT)]
            hT = A.h(KC, TOK)
            sqb = [A.h(2, TOK), A.h(2, TOK)]
            xr = [A.f(2, TOK), A.f(2, TOK)]
            rstd = A.f(TOK)
            tmpl = A.f(TOK)
            oA = A.h(4, TOK)
            oB = A.h(4, TOK)
            oC = A.h(4, TOK)
            oD = A.h(4, TOK)
            bufA = A.h(4, TOK)
            bufB = A.h(4, TOK)
            S1 = A.h(4, TOK)
            S2 = A.h(4, TOK)
            sg = [A.h(TOK), A.h(TOK)]
            f1 = A.f(TOK)
            f2 = A.f(TOK)
            f3 = A.f(TOK)
            t32 = A.f(4, 512)
            cin = A.f(4, TOK + 2)
            merged = A.h(KC, TOK)
            acc = [A.f(TOK), A.f(TOK)]
            xn = [A.f(TOK), A.f(TOK)]
            cosb = A.f(TOK)
            sinb = A.f(TOK)
            stat = A.f(4, 8)
            cst = A.f(4, 4)
            REG = A.off
            kb = [A.h(2, TOK), A.h(2, TOK)]
            vb = [A.h(4, 256), A.h(4, 256)]
            krb = [A.h(TOK), A.h(TOK)]
            pt = [A.h(TOK) for _ in range(4)]
            Lall = A.f(4, MC)
            LG = A.f(CH * 4)
            TBh = A.f(4, CH)
            INC = A.f(4, CH)
            negc = A.f(CH, 4)
            cref = A.f(4, NG)
            biasq = A.f(CH, 4)
            prevh = A.f(4 * NG * 2)
            REG_END = A.off
            A.off = REG
            s_k = A.h(4, PAST)
            s_v = A.h(8, 512)
            s_ck = A.h(2, PAST)
            s_kr = A.h(PAST)
            s_L9 = A.f(9, 4)
            s_TB = A.f(4, 9)
            s_INC = A.f(4, 9)
            s_negc = A.f(9, 4)
            s_bias = A.f(9, 4)
            s_pt = A.h(9, 16)
            s_flf = A.f(8, 4)
            s_cv = A.f(8)
            A.off = max(A.off, REG_END)

            wcount = [0]
            cvB = need_conv(("wbb", l))
            prevh4 = prevh.rearrange("p (c g t) -> p c g t", c=4, t=2)

            def wnext(row):
                s_ = wcount[0] % NSLOT
                wcount[0] += 1
                sk = ("ring", s_)
                dma("sp", ring[s_], wbb[row:row + P, :], [], [sk], dkey=sk, extra=cvB)
                return ring[s_], sk

            def prep():
                dma("sp", Lall, mout[l][:, :].rearrange("(r p) m -> p r m", p=P), [("mout", l)], ["Lall"], dkey="Lall")
                cp(LG.rearrange("p (g r c) -> p g r c", r=4, c=16),
                   Lall[:, :, 0:NG * 16].rearrange("p r (g c) -> p g r c", c=16), ["Lall"], ["LG"])
                bW = rot()
                mm(bW, psb[bW][:, 0:CH * 4], [(U32, LG)], ["U32", "LG"])
                bT = rot()
                mm(bT, psb[bT][:, 0:CH * 4], [(ones32, LG)], ["ones32", "LG"])
                cp(TBh, psb[bT][:, 0:CH * 4].rearrange("p (c h) -> p h c", h=4), [("ps", bT)], ["TBh"])
                multi("dve", [lambda e, h=h: e.tensor_tensor_scan(INC[:, h, :], onesCH[:, 0:CH], TBh[:, h, :], 0.0, ALU.mult, ALU.add)
                              for h in range(4)], ["TBh", "onesCH"], ["INC"])
                tt(TBh, TBh, INC, ALU.subtract, ["TBh", "INC"], ["TBh"])
                tt(negc, TBh.rearrange("p h c -> p c h"), psb[bW][:, 0:CH * 4].rearrange("p (c h) -> p c h", h=4), ALU.subtract,
                   ["TBh", ("ps", bW)], ["negc"])
                INCv = INC.rearrange("p h (g r s) -> p h g r s", r=4, s=4)
                ts(cref, INCv[:, :, :, 0, 3], sel[0], None, ALU.mult, None, ["INC", "sc"], ["cref"])
                for r in range(1, 4):
                    stt(cref, INCv[:, :, :, r, 3], sel[r], cref, ALU.mult, ALU.add, ["INC", "sc", "cref"], ["cref"])
                Hv = Lall[:, :, NG * 16:MC].rearrange("p r (c g t) -> p r c g t", c=4, t=2)
                ts(prevh4, Hv[:, 0], selP[0], None, ALU.mult, None, ["Lall", "sc"], ["prevh"])
                for r in (1, 2):
                    stt(prevh4, Hv[:, r], selP[r], prevh4, ALU.mult, ALU.add, ["Lall", "sc", "prevh"], ["prevh"])
                if NG > 1:
                    stt(prevh4[:, :, 1:NG, :], Hv[:, 3, :, 0:NG - 1, :], selP[3], prevh4[:, :, 1:NG, :], ALU.mult, ALU.add,
                        ["Lall", "sc", "prevh"], ["prevh"])

            def gsrc(kt, sec):
                r_k, g_k = kt % 4, kt // 4
                n = SECR[sec]
                return gout[l][g_k][sec][r_k * n:(r_k + 1) * n, :], ("gout", l, g_k, sec)

            tiles = [("s", 0)] + [("p", g) for g in range(NG)]
            for ti, (kind, g) in enumerate(tiles):
                N = TOK if kind == "p" else NS
                if l == 0:
                    bg_tick(3)
                if ti == 1:
                    S.barrier()
                    prep()
                if kind == "p":
                    rows = xsrc_p[g * P:(g + 1) * P, :]
                    rd_x = x_stores.get((xsrc_name, g), [])
                    make_hT_prompt(rows, hT, xr, sqb, rstd, tmpl, rd_x)
                    S.add("sp", lambda e, g=g: e.dma_start(out=cosb[0:64, :], in_=t_cos[g * 64:(g + 1) * 64, :]), [], ["cos"], dkey="cos")
                    S.add("sp", lambda e, g=g: e.dma_start(out=sinb[0:64, :], in_=t_sin[g * 64:(g + 1) * 64, :]), [], ["sin"], dkey="sin")
                else:
                    S.add("sp", lambda e: e.dma_start(out=cosb[0:64, 0:NS], in_=t_cos_s[:, :]), [], ["cos"], dkey="cos")
                    S.add("sp", lambda e: e.dma_start(out=sinb[0:64, 0:NS], in_=t_sin_s[:, :]), [], ["sin"], dkey="sin")
                    make_hT_sample(hT, sqb, rstd, tmpl)
                wrow = [l * NB * P]

                def W():
                    ap, key = wnext(wrow[0])
                    wrow[0] += P
                    return ap, key

                for ch in range(4):
                    w, wk = W()
                    bank, ps = proj_fm(w, hT, N, [wk, "hT"])
                    if kind == "p":
                        act(cin[:, ch, 2:2 + N], ps, AF.Identity, [("ps", bank)], ["cin"])
                    else:
                        for sq_ in range(2):
                            act(cin[:, ch, sq_ * 18 + 2:sq_ * 18 + 18], ps[:, sq_ * 16:(sq_ + 1) * 16], AF.Identity, [("ps", bank)], ["cin"])
                for ch in range(4):
                    w, wk = W()
                    bank, ps = proj_fm(w, hT, N, [wk, "hT"])
                    if kind == "p":
                        tt(cin[:, ch, 2:2 + N], cin[:, ch, 2:2 + N], ps, ALU.mult, [("ps", bank), "cin"], ["cin"])
                    else:
                        for sq_ in range(2):
                            tt(cin[:, ch, sq_ * 18 + 2:sq_ * 18 + 18], cin[:, ch, sq_ * 18 + 2:sq_ * 18 + 18], ps[:, sq_ * 16:(sq_ + 1) * 16],
                               ALU.mult, [("ps", bank), "cin"], ["cin"])
                if kind == "p":
                    cp(cin[:, :, 0:2], prevh4[:, :, g, :], ["prevh"], ["cin"])
                else:
                    for sq_ in range(2):
                        dma("sp", s_cv, c_conv[(l * 2 + sq_) * P:(l * 2 + sq_ + 1) * P, :], [], ["s_cv"], dkey="s_cv")
                        cp(cin[:, :, sq_ * 18:sq_ * 18 + 2], s_cv.rearrange("p (c t) -> p c t", t=2), ["s_cv"], ["cin"])
                        cp(cst[:, :, sq_ * 2:sq_ * 2 + 2], cin[:, :, sq_ * 18 + 16:sq_ * 18 + 18], ["cin"], ["cst"])
                    dma("pool", os_conv[l * P:(l + 1) * P, :].rearrange("p (c x) -> p c x", x=4), cst, ["cst"], [], dkey="cst")
                segs = [(0, N)] if kind == "p" else [(0, 16), (18, 16)]
                for ch in range(4):
                    w, wk = W()
                    bank_b, ps_b = proj_fm(w, hT, N, [wk, "hT"])
                    act(oD[:, ch, 0:N], ps_b, AF.Identity, [("ps", bank_b)], ["oD"])
                for ch in range(4):
                    w, wk = W()
                    bank_g, ps_g = proj_fm(w, hT, N, [wk, "hT"])
                    sgi = sg[ch % 2]
                    act(sgi[:, 0:N], ps_g, AF.Silu, [("ps", bank_g)], [("sg", ch % 2)])
                    for si, (c0, n) in enumerate(segs):
                        o0 = si * 16 if kind == "s" else 0
                        ts(f1[:, o0:o0 + n], cin[:, ch, c0:c0 + n], cw[:, ch, 0:1], None, ALU.mult, None, ["cin", LP], ["f1"])
                        stt(f1[:, o0:o0 + n], cin[:, ch, c0 + 1:c0 + 1 + n], cw[:, ch, 1:2], f1[:, o0:o0 + n], ALU.mult, ALU.add,
                            ["cin", LP, "f1"], ["f1"])
                        stt(f1[:, o0:o0 + n], cin[:, ch, c0 + 2:c0 + 2 + n], cw[:, ch, 2:3], f1[:, o0:o0 + n], ALU.mult, ALU.add,
                            ["cin", LP, "f1"], ["f1"])
                    tt(f1[:, 0:N], f1[:, 0:N], sgi[:, 0:N], ALU.mult, ["f1", ("sg", ch % 2)], ["f1"])
                    tt(oD[:, ch, 0:N], oD[:, ch, 0:N], f1[:, 0:N], ALU.mult, ["f1", "oD"], ["oD"])

                wv = [W() for _ in range(4)]
                nsub = 4 if kind == "p" else 2
                rows_ = P if kind == "p" else 16
                for sub in range(nsub):
                    bank = rot()
                    t0 = sub * rows_
                    for vc in range(4):
                        mm(bank, psb[bank][0:rows_, vc * P:(vc + 1) * P],
                           [(hT[:, kc, t0:t0 + rows_], wv[vc][0][:, kc * P:(kc + 1) * P]) for kc in range(KC)],
                           [wv[vc][1], "hT"])
                    cp(t32[0:rows_, sub, :], psb[bank][0:rows_, :], [("ps", bank)], ["t32"])
                    S.add("dve", lambda e, sub=sub, r_=rows_: e.bn_stats(stat[0:r_, sub, 0:6], t32[0:r_, sub, :]), ["t32"], ["stat"])
                    S.add("dve", lambda e, sub=sub, r_=rows_: e.bn_aggr(stat[0:r_, sub, 6:8], stat[0:r_, sub, 0:6]), ["stat"], ["stat"])
                    act(stat[0:rows_, sub, 7:8], stat[0:rows_, sub, 7:8], AF.Ln, ["stat"], ["stat"], bias=EPS)
                    act(stat[0:rows_, sub, 7:8], stat[0:rows_, sub, 7:8], AF.Exp, ["stat"], ["stat"], scale=-0.5)
                    ts(t32[0:rows_, sub, :], t32[0:rows_, sub, :], stat[0:rows_, sub, 6:7], stat[0:rows_, sub, 7:8], ALU.subtract, ALU.mult,
                       ["t32", "stat"], ["t32"])
                    tt(t32[0:rows_, sub, :], t32[0:rows_, sub, :], lng[0:rows_, :], ALU.mult, ["t32", LP], ["t32"])
                    tt(t32[0:rows_, sub, :], t32[0:rows_, sub, :], lnb[0:rows_, :], ALU.add, ["t32", LP], ["t32"])
                    cp(S1[0:rows_, sub, :], t32[0:rows_, sub, :], ["t32"], ["S1"])
                if kind == "p" and g == NG - 1:
                    dma("pool", o_gvn[l * P:(l + 1) * P, :], t32[:, 3, :], ["t32"], [], dkey="gvn")
                if kind == "s":
                    for sq_ in range(2):
                        dma("pool", os_gvn[(l * 2 + sq_) * 16:(l * 2 + sq_ + 1) * 16, :], t32[0:16, sq_, :], ["t32"], [], dkey="gvn")
                wu = [W() for _ in range(4)]
                for gr in range(4):
                    bank, ps = proj_fm(wu[gr][0], hT, N, [wu[gr][1], "hT"])
                    act(S2[:, gr, 0:N], ps, AF.Identity, [("ps", bank)], ["S2"])
                for gr in range(4):
                    w, wk = W()
                    bank, ps = proj_fm(w, hT, N, [wk, "hT"])
                    sgi = sg[gr % 2]
                    act(sgi[:, 0:N], ps, AF.Silu, [("ps", bank)], [("sg", gr % 2)])
                    tt(S2[:, gr, 0:N], S2[:, gr, 0:N], sgi[:, 0:N], ALU.mult, ["S2", ("sg", gr % 2)], ["S2"])
                    bank = rot()
                    if kind == "p":
                        for sub in range(4):
                            mm(bank, psb[bank][:, sub * P:(sub + 1) * P], [(S1[:, sub, gr * P:(gr + 1) * P], wmT[:, gr, :])], ["S1", LP])
                        for sub in range(4):
                            tt(f1[:, sub * P:(sub + 1) * P], psb[bank][:, sub * P:(sub + 1) * P], bsb[:, gr, :], ALU.add, [("ps", bank), LP], ["f1"])
                    else:
                        for sq_ in range(2):
                            mm(bank, psb[bank][:, sq_ * 16:(sq_ + 1) * 16], [(S1[0:16, sq_, gr * P:(gr + 1) * P], wmT[0:16, gr, 0:16])], ["S1", LP])
                        for sq_ in range(2):
                            tt(f1[:, sq_ * 16:(sq_ + 1) * 16], psb[bank][:, sq_ * 16:(sq_ + 1) * 16], bsb[:, gr, 0:16], ALU.add,
                               [("ps", bank), LP], ["f1"])
                    tt(oB[:, gr, 0:N], f1[:, 0:N], S2[:, gr, 0:N], ALU.mult, ["f1", "S2"], ["oB"])

                for h in range(4):
                    w, wk = W()
                    bank, ps = proj_fm(w, hT, N, [wk, "hT"])
                    act(bufA[:, h, 0:N], ps, AF.Identity, [("ps", bank)], ["bufA"])
                for h in range(4):
                    w, wk = W()
                    bank, ps = proj_fm(w, hT, N, [wk, "hT"])
                    act(bufB[:, h, 0:N], ps, AF.Silu, [("ps", bank)], ["bufB"])
                SC_A = float(128 ** -0.5)
                SC_C = float(192 ** -0.5)

                def finish_head(bO, bR, dst, dkey_, h, n0, n):
                    S.add("dve", lambda e, bR=bR, n0=n0, n=n: e.reciprocal(f1[:, n0:n0 + n], psb[bR][:, n0:n0 + n]), [("ps", bR)], ["f1"])
                    tt(f2[:, n0:n0 + n], psb[bO][:, n0:n0 + n], f1[:, n0:n0 + n], ALU.mult, [("ps", bO), "f1"], ["f2"])
                    tt(dst[:, h, n0:n0 + n], f2[:, n0:n0 + n], bufB[:, h, n0:n0 + n], ALU.mult, ["f2", "bufB"], [dkey_])

                def attn_prompt(is_mla, dst, dkey_):
                    nkt = 4 * g + 4
                    if not is_mla:
                        for h in range(4):
                            ts(biasq[:, 0:nkt * 4, h], negc[:, 0:nkt * 4, h], cref[:, h, g:g + 1], 0.0, ALU.add, ALU.min,
                               ["negc", "cref"], ["biasq"])
                        for r in range(4):
                            c0 = (4 * g + r) * 4
                            ts(biasq[:, c0:c0 + 4, :], biasq[:, c0:c0 + 4, :], kill[r], None, ALU.add, None, ["biasq", "sc"], ["biasq"])
                    ptc = [0]
                    kvc = [0]
                    LOOK = 2
                    for hp in range(2):
                        bO = [4, 5]
                        bR = [6, 7]
                        seq = []
                        for kt in range(nkt):
                            for hh_ in range(2):
                                for sub in range(4):
                                    seq.append(dict(kt=kt, hh=hh_, sub=sub, newtile=(hh_ == 0 and sub == 0)))
                        cur = {}

                        def emitS(c):
                            kt, hh_, sub = c["kt"], c["hh"], c["sub"]
                            if c["newtile"]:
                                bi = kvc[0] % 2
                                kvc[0] += 1
                                cur[kt] = bi
                                ksec, vsec = (2, 3) if is_mla else (0, 1)
                                src_, gk_ = gsrc(kt, ksec)
                                dma("sp", kb[bi], src_[:, hp * 1024:(hp + 1) * 1024].rearrange("p (a b) -> p a b", b=TOK),
                                    [gk_], [("kb", bi)], dkey=("kb", bi))
                                src_, gk_ = gsrc(kt, vsec)
                                dma("sp", vb[bi], src_.rearrange("p (a b) -> p a b", b=512)[:, :, hp * 256:(hp + 1) * 256],
                                    [gk_], [("vb", bi)], dkey=("vb", bi))
                                if is_mla:
                                    src_, gk_ = gsrc(kt, 4)
                                    dma("sp", krb[bi][0:64, :], src_.rearrange("a (b c) -> (a b) c", c=TOK),
                                        [gk_], [("krb", bi)], dkey=("krb", bi))
                            bi = cur[kt]
                            h = 2 * hp + hh_
                            bS = rot()
                            c["bS"] = bS
                            if is_mla:
                                mm(bS, psb[bS][:, :], [(kb[bi][:, hh_, sub * P:(sub + 1) * P], bufA[:, h, :]),
                                                      (krb[bi][0:64, sub * P:(sub + 1) * P], S1[0:64, h, :])],
                                   [("kb", bi), ("krb", bi), "bufA", "S1"])
                            else:
                                mm(bS, psb[bS][:, :], [(kb[bi][:, hh_, sub * P:(sub + 1) * P], bufA[:, h, :])],
                                   [("kb", bi), "bufA"])

                        def emitEP(c):
                            kt, hh_, sub, bS = c["kt"], c["hh"], c["sub"], c["bS"]
                            bi = cur[kt]
                            h = 2 * hp + hh_
                            r = kt - 4 * g
                            win = r >= 0
                            ci = kt * 4 + sub
                            pi = ptc[0] % 4
                            ptc[0] += 1
                            if is_mla:
                                act(pt[pi], psb[bS][:, :], AF.Exp, [("ps", bS), "sc"], [("pt", pi)],
                                    bias=(kill[r] if win else 0.0), scale=SC_C)
                            else:
                                act(pt[pi], psb[bS][:, :], AF.Exp, [("ps", bS), "biasq"], [("pt", pi)],
                                    bias=biasq[:, ci, h:h + 1], scale=SC_A)
                            if win:
                                msk = mm_ if is_mla else mf
                                stt(pt[pi], msk[:, sub, :], selA[r], pt[pi], ALU.max, ALU.mult, [("pt", pi), "sc", "mf", "mm"], [("pt", pi)])
                            first = (kt == 0 and sub == 0)
                            last = (kt == nkt - 1 and sub == 3)
                            mm_acc(bO[hh_], psb[bO[hh_]][:, :], [(vb[bi][:, sub, hh_ * P:(hh_ + 1) * P], pt[pi])],
                                   [("vb", bi), ("pt", pi)], first, last)
                            mm_acc(bR[hh_], psb[bR[hh_]][:, :], [(onesH, pt[pi])], ["onesH", ("pt", pi)], first, last)

                        for i in range(len(seq) + LOOK):
                            if i < len(seq):
                                emitS(seq[i])
                            if i - LOOK >= 0:
                                emitEP(seq[i - LOOK])
                        for hh_ in range(2):
                            finish_head(bO[hh_], bR[hh_], dst, dkey_, 2 * hp + hh_, 0, TOK)

                def attn_sample(is_mla, dst, dkey_):
                    for sq_ in range(2):
                        sl = (l * 2 + sq_)
                        q0 = sq_ * 16
                        if not is_mla:
                            dma("pool", s_k, c_fkT[sl * P:(sl + 1) * P, :].rearrange("p (a b) -> p a b", b=PAST), [], ["s_k"], dkey="s_k", cast=True)
                            dma("pool", s_v, c_fv[sl * P:(sl + 1) * P, :].rearrange("p (a b) -> p a b", b=512), [], ["s_v"], dkey="s_v", cast=True)
                            dma("sp", s_flf, c_flf[sl * P:(sl + 1) * P, :].rearrange("p (a b) -> p a b", b=4), [], ["s_flf"], dkey="s_flf")
                            S.add("dve", lambda e: e.memset(s_L9, 0.0), [], ["s_L9"])
                            cp(s_L9[:, 0:8, :], s_flf, ["s_flf"], ["s_L9"])
                            cp(s_L9[0:16, 8, :], lf_s[0:16, sq_, :], ["lf_s", "s_L9"], ["s_L9"])
                            L9f = s_L9.rearrange("p c h -> p (c h)")
                            bW_ = rot()
                            mm(bW_, psb[bW_][:, 0:36], [(U32, L9f)], ["U32", "s_L9"])
                            bT_ = rot()
                            mm(bT_, psb[bT_][:, 0:36], [(ones32, L9f)], ["ones32", "s_L9"])
                            cp(s_TB, psb[bT_][:, 0:36].rearrange("p (c h) -> p h c", h=4), [("ps", bT_)], ["s_TB"])
                            multi("dve", [lambda e, h=h: e.tensor_tensor_scan(s_INC[:, h, :], onesCH[:, 0:9], s_TB[:, h, :], 0.0, ALU.mult, ALU.add)
                                          for h in range(4)], ["s_TB", "onesCH"], ["s_INC"])
                            tt(s_TB, s_TB, s_INC, ALU.subtract, ["s_TB", "s_INC"], ["s_TB"])
                            tt(s_negc, s_TB.rearrange("p h c -> p c h"), psb[bW_][:, 0:36].rearrange("p (c h) -> p c h", h=4), ALU.subtract,
                               ["s_TB", ("ps", bW_)], ["s_negc"])
                            for h in range(4):
                                ts(s_bias[:, :, h], s_negc[:, :, h], s_INC[:, h, 8:9], 0.0, ALU.add, ALU.min, ["s_negc", "s_INC"], ["s_bias"])
                            knew, vnew = ksT, vS_s
                            rk = ["s_k", "ksT"]
                            rv = ["s_v", "vS_s"]
                        else:
                            dma("pool", s_ck, c_ckvT[sl * P:(sl + 1) * P, :].rearrange("p (a b) -> p a b", b=PAST), [], ["s_ck"], dkey="s_ck", cast=True)
                            dma("pool", s_kr[0:64, :], c_krT[sl * 64:(sl + 1) * 64, :], [], ["s_kr"], dkey="s_kr", cast=True)
                            for h in range(4):
                                for half in range(2):
                                    bank = rot()
                                    mm(bank, psb[bank][:, :], [(wkv[:, i, h * P:(h + 1) * P], s_ck[:, i, half * 512:(half + 1) * 512]) for i in range(2)],
                                       ["s_ck", LP])
                                    act(s_k[:, h, half * 512:(half + 1) * 512], psb[bank][:, :], AF.Identity, [("ps", bank)], ["s_k"])
                            for c in range(8):
                                bank = rot()
                                mm(bank, psb[bank][:, :], [(s_ck[:, i, c * P:(c + 1) * P], wkv[:, i, 512:1024]) for i in range(2)], ["s_ck", LP])
                                act(s_v[:, c, :], psb[bank][:, :], AF.Identity, [("ps", bank)], ["s_v"])
                            knew, vnew = knS_s, vcS_s
                            rk = ["s_k", "knS_s", "s_kr", "krS_s"]
                            rv = ["s_v", "vcS_s"]
                        for h in range(4):
                            bS = rot()
                            for c in range(9):
                                if c < 8:
                                    pairs = [(s_k[:, h, c * P:(c + 1) * P], bufA[:, h, q0:q0 + 16])]
                                    if is_mla:
                                        pairs.append((s_kr[0:64, c * P:(c + 1) * P], S1[0:64, h, q0:q0 + 16]))
                                    mm(bS, psb[bS][:, c * 16:(c + 1) * 16], pairs, rk + ["bufA", "S1"])
                                else:
                                    pairs = [(knew[:, h, q0:q0 + 16], bufA[:, h, q0:q0 + 16])]
                                    if is_mla:
                                        pairs.append((krS_s[0:64, q0:q0 + 16], S1[0:64, h, q0:q0 + 16]))
                                    mm(bS, psb[bS][0:16, c * 16:(c + 1) * 16], pairs, rk + ["bufA", "S1"])

                            def exps(e, h=h, bS=bS):
                                ins = None
                                for c in range(9):
                                    rws = P if c < 8 else 16
                                    if is_mla:
                                        ins = e.activation(s_pt[0:rws, c, :], psb[bS][0:rws, c * 16:(c + 1) * 16], AF.Exp, bias=0.0, scale=SC_C)
                                    else:
                                        ins = e.activation(s_pt[0:rws, c, :], psb[bS][0:rws, c * 16:(c + 1) * 16], AF.Exp,
                                                           bias=s_bias[0:rws, c, h:h + 1], scale=SC_A)
                                return ins
                            S.add("act", exps, [("ps", bS), "s_bias"], ["s_pt"])
                            if not is_mla:
                                tt(s_pt[0:16, 8, :], s_pt[0:16, 8, :], UH[0:16, 0:16], ALU.mult, ["s_pt", "UH"], ["s_pt"])
                            bO, bR = 4 + (h % 2), 6 + (h % 2)
                            prs = [(s_v[:, c, h * P:(h + 1) * P], s_pt[:, c, :]) for c in range(8)] + [(vnew[0:16, sq_, h * P:(h + 1) * P], s_pt[0:16, 8, :])]
                            mm(bO, psb[bO][:, q0:q0 + 16], prs, rv + ["s_pt"])
                            prs = [(onesH, s_pt[:, c, :]) for c in range(8)] + [(onesH[0:16, :], s_pt[0:16, 8, :])]
                            mm(bR, psb[bR][:, q0:q0 + 16], prs, ["onesH", "s_pt"])
                            finish_head(bO, bR, dst, dkey_, h, q0, 16)

                if kind == "p":
                    attn_prompt(False, oA, "oA")
                else:
                    attn_sample(False, oA, "oA")

                for i in range(3):
                    w, wk = W()
                    bank, ps = proj_fm(w, hT, N, [wk, "hT"])
                    cp(t32[:, i, 0:N], ps, [("ps", bank)], ["t32"])
                for h in range(4):
                    w, wk = W()
                    bank, ps = proj_fm(w, hT, N, [wk, "hT"])
                    act(bufB[:, h, 0:N], ps, AF.Silu, [("ps", bank)], ["bufB"])
                rms_sbuf(t32, 3, N, 384, sqb, rstd, tmpl, "t32")
                multi("dve", [lambda e, i=i, N=N: e.scalar_tensor_tensor(S2[:, i, 0:N], t32[:, i, 0:N], qg[:, i:i + 1], rstd[:, 0:N],
                                                                 ALU.mult, ALU.mult) for i in range(3)],
                      ["t32", "rstd", LP], ["S2"])
                for h in range(4):
                    bank = rot()
                    mm(bank, psb[bank][:, 0:N], [(wq[:, i, h * 256:h * 256 + P], S2[:, i, 0:N]) for i in range(3)], ["S2", LP])
                    act(bufA[:, h, 0:N], psb[bank][:, 0:N], AF.Identity, [("ps", bank)], ["bufA"])
                    bA_ = rot()
                    mm(bA_, psb[bA_][0:64, 0:N], [(wq[:, i, h * 256 + 128:h * 256 + 192], S2[:, i, 0:N]) for i in range(3)], ["S2", LP])
                    bB_ = rot()
                    mm(bB_, psb[bB_][0:64, 0:N], [(wq[:, i, h * 256 + 192:h * 256 + 256], S2[:, i, 0:N]) for i in range(3)], ["S2", LP])
                    tt(f1[0:64, 0:N], psb[bA_][0:64, 0:N], cosb[0:64, 0:N], ALU.mult, [("ps", bA_), "cos"], ["f1"])
                    tt(f2[0:64, 0:N], psb[bB_][0:64, 0:N], sinb[0:64, 0:N], ALU.mult, [("ps", bB_), "sin"], ["f2"])
                    tt(S1[0:64, h, 0:N], f1[0:64, 0:N], f2[0:64, 0:N], ALU.add, ["f1", "f2"], ["S1"])
                if kind == "p":
                    attn_prompt(True, oC, "oC")
                else:
                    attn_sample(True, oC, "oC")

                brs = [(oA, "oA"), (oB, "oB"), (oC, "oC"), (oD, "oD")]
                for fo in range(KC):
                    wg = [W() for _ in range(4)]
                    wp, wpk = W()
                    wpv = wp.rearrange("p (b k c) -> p b k c", b=4, k=4)
                    ai = fo % 2
                    for br in range(4):
                        bank, ps = proj_fm(wg[br][0], hT, N, [wg[br][1], "hT"])
                        sgi = sg[br % 2]
                        act(sgi[:, 0:N], ps, AF.Sigmoid, [("ps", bank)], [("sg", br % 2)])
                        b2 = rot()
                        mm(b2, psb[b2][:, 0:N], [(wpv[:, br, k, :], brs[br][0][:, k, 0:N]) for k in range(4)], [wpk, brs[br][1]])
                        if br == 0:
                            tt(acc[ai][:, 0:N], psb[b2][:, 0:N], sgi[:, 0:N], ALU.mult, [("ps", b2), ("sg", br % 2)], [("acc", ai)])
                        else:
                            tt(f3[:, 0:N], psb[b2][:, 0:N], sgi[:, 0:N], ALU.mult, [("ps", b2), ("sg", br % 2)], ["f3"])
                            if br < 3:
                                tt(acc[ai][:, 0:N], acc[ai][:, 0:N], f3[:, 0:N], ALU.add, [("acc", ai), "f3"], [("acc", ai)])
                            else:
                                tt(merged[:, fo, 0:N], acc[ai][:, 0:N], f3[:, 0:N], ALU.add, [("acc", ai), "f3"], ["merged"])
                sts = []
                for fo in range(KC):
                    w, wk = W()
                    bank, ps = proj_fm(w, merged, N, [wk, "merged"])
                    if kind == "p":
                        xi = fo % 2
                        xk = ("xr", xi)
                        dma("sp", xr[xi][:, 0, :], xsrc_p[g * P:(g + 1) * P, fo * TOK:(fo + 1) * TOK], [], [xk], dkey=xk, extra=rd_x)
                        tt(xn[xi], ps, xr[xi][:, 0, :], ALU.add, [("ps", bank), xk], [("xn", xi)])
                        sts.append(dma("pool", xdst_p[g * P:(g + 1) * P, fo * TOK:(fo + 1) * TOK], xn[xi], [("xn", xi)], [], dkey=("xn", xi)))
                    else:
                        tt(xs_cur[:, fo, :], ps, xs_cur[:, fo, :], ALU.add, [("ps", bank), "xs_cur"], ["xs_cur"])
                if kind == "p":
                    x_stores[(xdst_name, g)] = sts

        def pass_C():
            A.off = PERSIST
            S.barrier()
            xr = [A.f(2, TOK), A.f(2, TOK)]
            yt = [A.f(2, TOK), A.f(2, TOK)]
            ysm = A.f(KC, NS)
            sqb = [A.h(2, TOK), A.h(2, TOK)]
            rstd = A.f(TOK)
            tmpl = A.f(TOK)
            rms_sbuf(xs_cur, KC, NS, DM, sqb, rstd, tmpl, "xs_cur")
            multi("dve", [lambda e, kc=kc: e.scalar_tensor_tensor(ysm[:, kc, :], xs_cur[:, kc, :], fgam[:, kc:kc + 1], rstd[:, 0:NS],
                                                                  ALU.mult, ALU.mult) for kc in range(KC)],
                  ["xs_cur", "rstd", "fgam"], ["ysm"])
            dma("pool", ys[:, :].rearrange("p (a b) -> p a b", b=NS), ysm, ["ysm"], [], dkey="ysm")
            for g in range(NG):
                rows = x2p[g * P:(g + 1) * P, :]
                rd = x_stores.get(("x2p", g), [])
                rms_stream(rows, xr, sqb, rstd, tmpl, rd)
                for i in range(8):
                    xk = ("xr", i % 2)
                    yk = ("yt", i % 2)
                    dma("sp", xr[i % 2], rows[:, i * 2 * TOK:(i + 1) * 2 * TOK].rearrange("p (a b) -> p a b", b=TOK), [], [xk], dkey=xk, extra=rd)
                    multi("dve", [lambda e, i=i, k=k: e.scalar_tensor_tensor(yt[i % 2][:, k, :], xr[i % 2][:, k, :], fgam[:, 2 * i + k:2 * i + k + 1],
                                                                             rstd[:, :], ALU.mult, ALU.mult) for k in range(2)],
                          [xk, "rstd", "fgam"], [yk])
                    dma("pool", yp[g * P:(g + 1) * P, i * 2 * TOK:(i + 1) * 2 * TOK].rearrange("p (a b) -> p a b", b=TOK), yt[i % 2], [yk], [], dkey=yk)

        import os as _os
        _stop = _os.environ.get("KSTOP", "")
        pass_A(0, xp, "xp")
        if _stop != "A0":
            pass_B(0, xp, "xp", x1p, "x1p")
            if _stop != "B0":
                pass_A(1, x1p, "x1p")
                pass_B(1, x1p, "x1p", x2p, "x2p")
                pass_C()
        bg_tick()

        S.finalize()
        with nc.Block() as block:
            @block.tensor
            def _(e):
                S.emit("pe", e, semof)

            @block.scalar
            def _(e):
                S.emit("act", e, semof)

            @block.vector
            def _(e):
                S.emit("dve", e, semof)

            @block.gpsimd
            def _(e):
                S.emit("pool", e, semof)

            @block.sync
            def _(e):
                S.emit("sp", e, semof)
    return nc, S


def _chunkify(W):
    K, n = W.shape
    kc = K // P
    out = np.zeros((P, kc, P), np.float32)
    out[:, :, :n] = W.reshape(kc, P, n).transpose(1, 0, 2)
    return out.reshape(P, kc * P)


def _fm(x):
    T, Fd = x.shape
    return np.ascontiguousarray(x.T.reshape(Fd // P, P, T).transpose(1, 0, 2)).reshape(P, -1)


def _unfm(a, T):
    kc = a.shape[1] // T
    return np.ascontiguousarray(a.reshape(P, kc, T).transpose(2, 1, 0)).reshape(T, kc * P)


def _rep(v):
    return np.ascontiguousarray(np.broadcast_to(np.asarray(v, np.float32).reshape(1, -1), (P, v.size)))


_PROG = {}


def kernel(x_prompt, x_sample, cache_fox_k, cache_fox_v, cache_fox_logf, cache_mla_ckv, cache_mla_krope,
           state_conv, norm_g, w_in, fox_fb, gm_ln_g, gm_ln_b, gm_ws, gm_bs, mla_q_norm_g, mla_wq_b,
           mla_kv_norm_g, mla_wkv_b, conv_w, w_branch, w_out, final_norm_g):
    f32 = np.float32
    x_prompt = np.asarray(x_prompt, f32)
    x_sample = np.asarray(x_sample, f32)
    B, SEQ, _ = x_prompt.shape
    NG = SEQ // (4 * TOK)
    assert B == 2 and NG * 4 * TOK == SEQ
    w_in = np.asarray(w_in, f32)
    w_branch = np.asarray(w_branch, f32)
    w_out = np.asarray(w_out, f32)
    mla_wq_b = np.asarray(mla_wq_b, f32)
    mla_wkv_b = np.asarray(mla_wkv_b, f32)
    waf = np.zeros((L, NA, P, 2048), f32)
    wbf = np.zeros((L, NB, P, 2048), f32)
    wqf = np.zeros((L, P, 3, 1024), f32)
    wkvf = np.zeros((L, P, 2, 1024), f32)
    for l in range(L):
        wi = w_in[l]

        def ch(name, i, n=P):
            o = OFF[name] + i * P
            return _chunkify(wi[:, o:o + n])
        for h in range(4):
            waf[l, h] = ch("a_k", h)
            waf[l, 4 + h] = ch("a_v", h)
        waf[l, 8] = ch("c_kv", 0)
        waf[l, 9] = ch("c_kv", 1)
        o = OFF["c_kr"]
        kr = wi[:, o:o + 64]
        waf[l, 10] = _chunkify(np.concatenate([kr, kr[:, 32:64], kr[:, 0:32]], axis=1))
        waf[l, 11] = ch("a_f", 0, 4)
        for i in range(4):
            waf[l, 12 + i] = ch("d_c", i)
            waf[l, 16 + i] = ch("d_h", i)
        seq = []
        for nm, cnt in (("d_c", 4), ("d_h", 4), ("d_b", 4), ("d_g", 4), ("b_v", 4), ("b_u", 4), ("b_g", 4),
                        ("a_q", 4), ("a_g", 4), ("c_q", 3), ("c_g", 4)):
            for i in range(cnt):
                seq.append(ch(nm, i))
        for fo in range(KC):
            for br in range(4):
                o = OFF["m_g"] + br * DM + fo * P
                seq.append(_chunkify(wi[:, o:o + P]))
            pk = np.stack([w_branch[l, br][:, fo * P:(fo + 1) * P].reshape(4, P, P).transpose(1, 0, 2) for br in range(4)], axis=1)
            seq.append(pk.reshape(P, 2048))
        for fo in range(KC):
            seq.append(_chunkify(w_out[l][:, fo * P:(fo + 1) * P]))
        assert len(seq) == NB
        wbf[l] = np.stack(seq)
        wqc = []
        for h in range(4):
            b0 = h * 192
            wqc += [mla_wq_b[l][:, b0:b0 + 128], mla_wq_b[l][:, b0 + 128:b0 + 192], mla_wq_b[l][:, b0 + 160:b0 + 192],
                    mla_wq_b[l][:, b0 + 128:b0 + 160]]
        wqc = np.concatenate(wqc, axis=1)
        wqf[l] = wqc.reshape(3, P, 1024).transpose(1, 0, 2)
        wk = np.concatenate([mla_wkv_b[l][:, h * 256:h * 256 + 128] for h in range(4)] +
                            [mla_wkv_b[l][:, h * 256 + 128:h * 256 + 256] for h in range(4)], axis=1)
        wkvf[l] = wk.reshape(2, P, 1024).transpose(1, 0, 2)
    waf = waf.reshape(L * NA * P, 2048)
    wbf = wbf.reshape(L * NB * P, 2048)
    wqf = wqf.reshape(L * P, 3 * 1024)
    wkvf = wkvf.reshape(L * P, 2 * 1024)
    norm_g = np.asarray(norm_g, f32)
    p_ng = np.concatenate([norm_g[l].reshape(KC, P).T for l in range(L)], axis=0)
    p_fb = np.concatenate([_rep(np.tile(np.asarray(fox_fb, f32)[l], 4)) for l in range(L)], axis=0)
    p_lng = np.concatenate([_rep(np.asarray(gm_ln_g, f32)[l]) for l in range(L)], axis=0)
    p_lnb = np.concatenate([_rep(np.asarray(gm_ln_b, f32)[l]) for l in range(L)], axis=0)
    gm_ws = np.asarray(gm_ws, f32)
    gm_bs = np.asarray(gm_bs, f32)
    p_wmT = np.concatenate([gm_ws[l].transpose(2, 0, 1).reshape(P, 512) for l in range(L)], axis=0)
    p_bs = np.concatenate([_rep(gm_bs[l].reshape(-1)) for l in range(L)], axis=0)
    p_qg = np.concatenate([np.asarray(mla_q_norm_g, f32)[l].reshape(3, P).T for l in range(L)], axis=0)
    p_kvg = np.concatenate([np.asarray(mla_kv_norm_g, f32)[l].reshape(2, P).T for l in range(L)], axis=0)
    conv_w = np.asarray(conv_w, f32)
    p_cw = np.concatenate([conv_w[l].T.reshape(4, P, 3).transpose(1, 0, 2).reshape(P, 12) for l in range(L)], axis=0)
    p_fg = np.asarray(final_norm_g, f32).reshape(KC, P).T
    half = 32
    inv = (10000.0 ** (-np.arange(half, dtype=f32) / half)).astype(f32)

    def rope_tab(pos):
        ang = pos.astype(f32)[:, None] * inv[None, :]
        c, s = np.cos(ang).astype(f32), np.sin(ang).astype(f32)
        return np.concatenate([c, c], axis=1).T.copy(), np.concatenate([-s, s], axis=1).T.copy()
    cs_s, sn_s = rope_tab(PAST + np.arange(16))
    t_cos_s = np.concatenate([cs_s, cs_s], axis=1)
    t_sin_s = np.concatenate([sn_s, sn_s], axis=1)
    kU = (np.arange(P)[:, None] <= np.arange(P)[None, :]).astype(f32)
    pk = (np.arange(4)[None, :, None] * P + np.arange(P)[:, None, None])
    qi = np.arange(TOK)[None, None, :]
    mf = (pk <= qi).astype(f32).reshape(P, 2048).astype(ml_dtypes.bfloat16)
    mmk = ((pk // 64) <= (qi // 64)).astype(f32).reshape(P, 2048).astype(ml_dtypes.bfloat16)

    cfk = np.asarray(cache_fox_k, f32)
    cfv = np.asarray(cache_fox_v, f32)
    cfl = np.asarray(cache_fox_logf, f32)
    cck = np.asarray(cache_mla_ckv, f32)
    ckr = np.asarray(cache_mla_krope, f32)
    scv = np.asarray(state_conv, f32)

    in_maps = []
    for c in range(8):
        b, j = c // 4, c % 4
        xpt = np.concatenate([_fm(x_prompt[b, (4 * g + j) * TOK:(4 * g + j + 1) * TOK]) for g in range(NG)], axis=0)
        xst = _fm(x_sample[2 * c:2 * c + 2].reshape(NS, DM))
        tc, tsn = [], []
        for g in range(NG):
            cc_, ss_ = rope_tab((4 * g + j) * TOK + np.arange(TOK))
            tc.append(cc_)
            tsn.append(ss_)
        ksc = np.zeros((P, 16), f32)
        for r in range(4):
            ksc[:, r] = 1.0 if r < j else 0.0
            ksc[:, 4 + r] = NEG_KILL if r > j else 0.0
            ksc[:, 8 + r] = 1.0 if r == j else 0.0
        for r in range(3):
            ksc[:, 12 + r] = 1.0 if j == r + 1 else 0.0
        ksc[:, 15] = 1.0 if j == 0 else 0.0
        sq = [2 * c, 2 * c + 1]
        m = dict(
            xp=xpt, xs=xst, waf=waf, wbf=wbf, wqf=wqf, wkvf=wkvf,
            c_fkT=np.concatenate([cfk[l, s].transpose(2, 1, 0).reshape(P, 4 * PAST) for l in range(L) for s in sq], axis=0),
            c_fv=np.concatenate([cfv[l, s].reshape(8, P, 512).transpose(1, 0, 2).reshape(P, 8 * 512) for l in range(L) for s in sq], axis=0),
            c_flf=np.concatenate([cfl[l, s].reshape(8, P, 4).transpose(1, 0, 2).reshape(P, 32) for l in range(L) for s in sq], axis=0),
            c_ckvT=np.concatenate([cck[l, s].T.reshape(2, P, PAST).transpose(1, 0, 2).reshape(P, 2 * PAST) for l in range(L) for s in sq], axis=0),
            c_krT=np.concatenate([ckr[l, s].T for l in range(L) for s in sq], axis=0),
            c_conv=np.concatenate([scv[l, s].T.reshape(4, P, 2).transpose(1, 0, 2).reshape(P, 8) for l in range(L) for s in sq], axis=0),
            p_ng=p_ng, p_fb=p_fb, p_lng=p_lng, p_lnb=p_lnb, p_wmT=p_wmT, p_bs=p_bs, p_qg=p_qg, p_kvg=p_kvg,
            p_cw=p_cw, p_fg=p_fg,
            t_cos=np.concatenate(tc, axis=0), t_sin=np.concatenate(tsn, axis=0), t_cos_s=t_cos_s, t_sin_s=t_sin_s,
            k_mf=mf, k_mm=mmk, k_U=kU, k_sc=ksc,
        )
        in_maps.append({k: np.ascontiguousarray(v) for k, v in m.items()})

    if NG not in _PROG:
        _PROG[NG] = build_program(NG)[0]
    nc = _PROG[NG]
    res = run_bass_kernel_spmd(nc, in_maps, core_ids=list(range(8))).results

    KEEP = min(SEQ, PAST)
    y_prompt = np.zeros((B, SEQ, DM), f32)
    y_sample = np.zeros((16, 16, DM), f32)
    p_fox_k = np.zeros((L, B, KEEP, 4, 128), f32)
    p_fox_v = np.zeros((L, B, KEEP, 4, 128), f32)
    p_fox_logf = np.zeros((L, B, KEEP, 4), f32)
    p_mla_ckv = np.zeros((L, B, KEEP, 256), f32)
    p_mla_krope = np.zeros((L, B, KEEP, 64), f32)
    p_conv = np.zeros((L, B, 2, 512), f32)
    p_gmlp_v = np.zeros((L, B, 128, 512), f32)
    s_fox_k = np.zeros((L, 16, 16, 4, 128), f32)
    s_fox_v = np.zeros((L, 16, 16, 4, 128), f32)
    s_fox_logf = np.zeros((L, 16, 16, 4), f32)
    s_mla_ckv = np.zeros((L, 16, 16, 256), f32)
    s_mla_krope = np.zeros((L, 16, 16, 64), f32)
    s_conv = np.zeros((L, 16, 2, 512), f32)
    s_gmlp_v = np.zeros((L, 16, 16, 512), f32)
    for c in range(8):
        b, j = c // 4, c % 4
        r = res[c]
        for g in range(NG):
            T = 4 * g + j
            y_prompt[b, T * TOK:(T + 1) * TOK] = _unfm(r["yp"][g * P:(g + 1) * P], TOK)
        y_sample[2 * c:2 * c + 2] = _unfm(r["ys"], NS).reshape(2, 16, DM)
        for l in range(L):
            if j >= 2:
                k0 = (j - 2) * TOK
                p_fox_k[l, b, k0:k0 + TOK] = r["o_fkT"][l * P:(l + 1) * P].reshape(P, 4, TOK).transpose(2, 1, 0)
                p_fox_v[l, b, k0:k0 + TOK] = r["o_fv"][l * P:(l + 1) * P].reshape(P, 4, 4, 128).transpose(1, 0, 2, 3).reshape(TOK, 4, 128)
                p_fox_logf[l, b, k0:k0 + TOK] = r["o_flf"][l * P:(l + 1) * P].reshape(P, 4, 4).transpose(1, 0, 2).reshape(TOK, 4)
                p_mla_ckv[l, b, k0:k0 + TOK] = r["o_ckvT"][l * P:(l + 1) * P].reshape(P, 2, TOK).transpose(2, 1, 0).reshape(TOK, 256)
                p_mla_krope[l, b, k0:k0 + TOK] = r["o_krT"][l * 64:(l + 1) * 64].T
            if j == 3:
                hl = r["o_halo"][l * P:(l + 1) * P].reshape(P, 4, NG, 2)[:, :, NG - 1, :]
                p_conv[l, b] = hl.transpose(2, 1, 0).reshape(2, 512)
                p_gmlp_v[l, b] = r["o_gvn"][l * P:(l + 1) * P]
            for sq_ in range(2):
                s = 2 * c + sq_
                s_fox_k[l, s] = r["os_fkT"][l * P:(l + 1) * P].reshape(P, 4, NS)[:, :, sq_ * 16:(sq_ + 1) * 16].transpose(2, 1, 0)
                s_fox_v[l, s] = r["os_fv"][(l * 2 + sq_) * 16:(l * 2 + sq_ + 1) * 16].reshape(16, 4, 128)
                s_fox_logf[l, s] = r["os_flf"][(l * 2 + sq_) * 16:(l * 2 + sq_ + 1) * 16]
                s_mla_ckv[l, s] = r["os_ckvT"][l * P:(l + 1) * P].reshape(P, 2, NS)[:, :, sq_ * 16:(sq_ + 1) * 16].transpose(2, 1, 0).reshape(16, 256)
                s_mla_krope[l, s] = r["os_krT"][l * 64:(l + 1) * 64][:, sq_ * 16:(sq_ + 1) * 16].T
                s_conv[l, s] = r["os_conv"][l * P:(l + 1) * P].reshape(P, 4, 2, 2)[:, :, sq_, :].transpose(2, 1, 0).reshape(2, 512)
                s_gmlp_v[l, s] = r["os_gvn"][(l * 2 + sq_) * 16:(l * 2 + sq_ + 1) * 16]
    return (y_prompt, y_sample, p_fox_k, p_fox_v, p_fox_logf, p_mla_ckv, p_mla_krope, p_conv, p_gmlp_v,
            s_fox_k, s_fox_v, s_fox_logf, s_mla_ckv, s_mla_krope, s_conv, s_gmlp_v)
```
